# Optimizing a Trainium2 kernel written in Bass

```python
import jax, jax.numpy as jnp
from jax import lax
import numpy as np

D_MODEL = 1024
BATCH = 8
SEQ = 2048
DEPTH = 1
DEC_BATCH = 128
DEC_SEQ = 8
PAST_LEN = 16384
PAGE_SIZE = 128

D_MIX = D_MODEL
D_A = D_MIX // 2
D_B = D_MIX - D_A
GROUP_WIDTH = 64
N_GROUPS_A = D_A // GROUP_WIDTH
N_GROUPS_B = D_B // GROUP_WIDTH
K_A = 3
K_B = 31
K_F = 3
D_FF = 2816
D_PLE = 256
EPS = 1e-6

kernel_name = "hybrid_conv_decoder_step"


def _rmsnorm(x, g):
    xf = x.astype(jnp.float32)
    y = xf * lax.rsqrt(jnp.mean(xf * xf, axis=-1, keepdims=True) + EPS)
    return (y * g.astype(jnp.float32)).astype(x.dtype)


def _layernorm(x, g, b):
    xf = x.astype(jnp.float32)
    mu = jnp.mean(xf, axis=-1, keepdims=True)
    xc = xf - mu
    var = jnp.mean(xc * xc, axis=-1, keepdims=True)
    y = xc * lax.rsqrt(var + EPS) * g.astype(jnp.float32) + b.astype(jnp.float32)
    return y.astype(x.dtype)


def _causal_dwconv(x, hist, w):
    k = w.shape[0]
    xp = jnp.concatenate([hist.astype(x.dtype), x], axis=1)
    y = lax.conv_general_dilated(
        xp, w[:, None, :].astype(x.dtype), window_strides=(1,), padding="VALID",
        dimension_numbers=("NWC", "WIO", "NWC"), feature_group_count=x.shape[-1])
    return y, xp[:, xp.shape[1] - (k - 1):]


def _layer(h, p, st_a, st_b, st_f,
           g_mix, w_in, conv_a_w, conv_b_w, conv_b_b, ln_b_g, ln_b_b, w_out,
           g_ffn, w_up, conv_f_w, conv_f_b, w_down, g_ple, w_ple, w_ple_gate):
    u = _rmsnorm(h, g_mix)
    z = jnp.einsum("btd,de->bte", u, w_in)
    b_gate, c_gate, v, glu_a, glu_g = jnp.split(
        z, [D_A, 2 * D_A, 3 * D_A, 3 * D_A + D_B], axis=-1)
    ya, new_a = _causal_dwconv(c_gate * v, st_a, conv_a_w)
    ya = b_gate * ya
    gl = glu_a * jax.nn.sigmoid(glu_g)
    yb, new_b = _causal_dwconv(gl, st_b, conv_b_w)
    yb = jax.nn.silu(_layernorm(yb + conv_b_b, ln_b_g, ln_b_b))
    h = h + jnp.einsum("bte,ed->btd", jnp.concatenate([ya, yb], axis=-1), w_out)
    u = _rmsnorm(h, g_ffn)
    up = jnp.einsum("btd,df->btf", u, w_up)
    gate, val = jnp.split(up, 2, axis=-1)
    gate, new_f = _causal_dwconv(gate, st_f, conv_f_w)
    f = jax.nn.silu(gate + conv_f_b) * val
    h = h + jnp.einsum("btf,fd->btd", f, w_down)
    gate_p = jax.nn.sigmoid(jnp.einsum("btd,de->bte", _rmsnorm(h, g_ple), w_ple_gate))
    h = h + jnp.einsum("btp,pd->btd", p, w_ple) * gate_p
    return h, new_a, new_b, new_f


def setup_inputs(seed: int = 0) -> dict:
    key = jax.random.key(seed)
    ks = jax.random.split(key, 24)
    n = lambda k, shape, s: jax.random.normal(k, shape, jnp.float32) * s
    L = DEPTH
    return {
        "x_prompt": n(ks[0], (BATCH, SEQ, D_MODEL), 1.0),
        "x_sample": n(ks[1], (DEC_BATCH, DEC_SEQ, D_MODEL), 1.0),
        "p_prompt": n(ks[2], (L, BATCH, SEQ, D_PLE), 1.0),
        "p_sample": n(ks[3], (L, DEC_BATCH, DEC_SEQ, D_PLE), 1.0),
        "state_conv_a": n(ks[4], (L, DEC_BATCH, K_A - 1, D_A), 1.0),
        "state_conv_b": n(ks[5], (L, DEC_BATCH, K_B - 1, D_B), 0.5),
        "state_ffn_conv": n(ks[6], (L, DEC_BATCH, K_F - 1, D_FF), 1.0),
        "g_mix": 1.0 + n(ks[7], (L, D_MODEL), 0.02),
        "w_in": n(ks[8], (L, D_MODEL, 3 * D_A + 2 * D_B), D_MODEL ** -0.5),
        "conv_a_w": n(ks[9], (L, K_A, D_A), K_A ** -0.5),
        "conv_b_w": n(ks[10], (L, K_B, D_B), K_B ** -0.5),
        "conv_b_b": n(ks[11], (L, D_B), 0.02),
        "ln_b_g": 1.0 + n(ks[12], (L, D_B), 0.02),
        "ln_b_b": n(ks[13], (L, D_B), 0.02),
        "w_out": n(ks[14], (L, D_MIX, D_MODEL), D_MIX ** -0.5),
        "g_ffn": 1.0 + n(ks[15], (L, D_MODEL), 0.02),
        "w_up": n(ks[16], (L, D_MODEL, 2 * D_FF), D_MODEL ** -0.5),
        "conv_f_w": n(ks[17], (L, K_F, D_FF), K_F ** -0.5),
        "conv_f_b": n(ks[18], (L, D_FF), 0.02),
        "w_down": n(ks[19], (L, D_FF, D_MODEL), D_FF ** -0.5),
        "g_ple": 1.0 + n(ks[20], (L, D_MODEL), 0.02),
        "w_ple": n(ks[21], (L, D_PLE, D_MODEL), D_PLE ** -0.5),
        "w_ple_gate": n(ks[22], (L, D_MODEL, D_MODEL), D_MODEL ** -0.5),
        "g_final": 1.0 + n(ks[23], (D_MODEL,), 0.02),
    }


def reference(x_prompt, x_sample, p_prompt, p_sample, state_conv_a, state_conv_b, state_ffn_conv,
              g_mix, w_in, conv_a_w, conv_b_w, conv_b_b, ln_b_g, ln_b_b, w_out,
              g_ffn, w_up, conv_f_w, conv_f_b, w_down, g_ple, w_ple, w_ple_gate, g_final):
    hp, hs = x_prompt, x_sample
    nb = x_prompt.shape[0]
    dt = x_prompt.dtype
    pa, pb, pf, sa, sb, sf = [], [], [], [], [], []
    for i in range(DEPTH):
        w = (g_mix[i], w_in[i], conv_a_w[i], conv_b_w[i], conv_b_b[i], ln_b_g[i], ln_b_b[i],
             w_out[i], g_ffn[i], w_up[i], conv_f_w[i], conv_f_b[i], w_down[i],
             g_ple[i], w_ple[i], w_ple_gate[i])
        hp, a1, b1, f1 = _layer(hp, p_prompt[i],
                                jnp.zeros((nb, K_A - 1, D_A), dt),
                                jnp.zeros((nb, K_B - 1, D_B), dt),
                                jnp.zeros((nb, K_F - 1, D_FF), dt), *w)
        hs, a2, b2, f2 = _layer(hs, p_sample[i], state_conv_a[i], state_conv_b[i],
                                state_ffn_conv[i], *w)
        pa.append(a1); pb.append(b1); pf.append(f1)
        sa.append(a2); sb.append(b2); sf.append(f2)
    y_prompt = _rmsnorm(hp, g_final)
    y_sample = _rmsnorm(hs, g_final)
    return (y_prompt, y_sample,
            jnp.stack(pa), jnp.stack(pb), jnp.stack(pf),
            jnp.stack(sa), jnp.stack(sb), jnp.stack(sf))
```

```python
import numpy as np
import concourse.bass as bass
import concourse.mybir as mybir
from concourse.bass_utils import run_bass_kernel_spmd

F32 = mybir.dt.float32
BF16 = mybir.dt.bfloat16
ALU = mybir.AluOpType
AF = mybir.ActivationFunctionType

NCORES = 8
D = 1024
KC = 8
DFF = 2816
NJF = 22
NJH = 11
T = 1088
NPASS = 2
EPS = 1e-6
TBW = 544
NTB = 11
NPE = 31
NPRM = 268

P_G1, P_G2, P_G3, P_GF = 0, 8, 16, 24
P_WA = 32
P_WB = 44
P_CBB = 168
P_LNG = 172
P_LNB = 176
P_WF = 180
P_CBF = 246


class Seg:
    def __init__(self, kind, n, off, idx):
        self.kind = kind
        self.n = n
        self.off = off
        self.idx = idx

    def cv(self, ap2d):
        if self.kind == 'p':
            return ap2d
        return ap2d.rearrange("p (s t) -> p s t", t=8)

    def _b3(self, buf, H):
        return buf[:, 0:8 * (H + 8)].rearrange("p (s w) -> p s w", w=H + 8)

    def body(self, buf, H):
        if self.kind == 'p':
            return buf[:, H:H + self.n]
        return self._b3(buf, H)[:, :, H:H + 8]

    def tap(self, buf, H, k):
        if self.kind == 'p':
            return buf[:, k:k + self.n]
        return self._b3(buf, H)[:, :, k:k + 8]

    def head(self, buf, H):
        if self.kind == 'p':
            return buf[:, 0:H]
        return self._b3(buf, H)[:, :, 0:H]

    def tail(self, buf, H):
        if self.kind == 'p':
            return buf[:, self.n:self.n + H]
        return self._b3(buf, H)[:, :, 8:8 + H]


SEGS = [Seg('p', 512, 0, 0), Seg('p', 512, 512, 1), Seg('s', 64, 1024, 2)]


class Prog:
    ENG = ('pe', 'dve', 'act', 'pool', 'sp')

    def __init__(self, nc):
        self.nc = nc
        self.streams = {e: [] for e in self.ENG}
        self.semh = {}
        self.esem = {}
        self.ecnt = {}
        for e in ('pe', 'dve', 'act', 'pool'):
            name = "sem_" + e
            self.semh[name] = nc.alloc_semaphore(name)
            self.esem[e] = name
            self.ecnt[e] = 0
        self.waited = {e: {} for e in self.ENG}
        self.lastw = {}
        self.readers = {}
        self.dsems = {}
        self.tag = ""
        self.pe_labels = []

    def dsem(self, key):
        if key not in self.dsems:
            name = "dsem_%d" % len(self.dsems)
            self.semh[name] = self.nc.alloc_semaphore(name)
            self.dsems[key] = [name, 0]
        return self.dsems[key]

    def op(self, eng, fn, reads=(), writes=(), dma=None):
        deps = {}

        def need(tok):
            s, v = tok
            if deps.get(s, 0) < v:
                deps[s] = v

        for k in reads:
            if k in self.lastw:
                need(self.lastw[k])
        for k in writes:
            if k in self.lastw:
                need(self.lastw[k])
            for s, v in self.readers.get(k, {}).items():
                need((s, v))
        waits = []
        for s, v in deps.items():
            if eng == 'pe' and s == self.esem['pe']:
                continue
            if self.waited[eng].get(s, 0) >= v:
                continue
            self.waited[eng][s] = v
            waits.append((s, v))
        if dma is not None:
            ds = self.dsem(dma)
            ds[1] += 16
            tok = (ds[0], ds[1])
            inc = (ds[0], 16)
        else:
            self.ecnt[eng] += 1
            tok = (self.esem[eng], self.ecnt[eng])
            inc = (self.esem[eng], 1)
        self.streams[eng].append((fn, waits, inc, self.tag))
        for k in reads:
            r = self.readers.setdefault(k, {})
            if r.get(tok[0], 0) < tok[1]:
                r[tok[0]] = tok[1]
        for k in writes:
            self.lastw[k] = tok
            self.readers[k] = {}
        return tok

    def fence_all_dma(self):
        waits = [(name, cnt) for (name, cnt) in self.dsems.values() if cnt > 0]
        self.streams['sp'].append((None, waits, None, 'fence'))

    def replay(self, name, e):
        for fn, waits, inc, tag in self.streams[name]:
            for s, v in waits:
                e.wait_ge(self.semh[s], v)
            if fn is None:
                continue
            if name == 'pe':
                e = _Count(e, self.pe_labels, tag)
            ins = fn(e)
            if name == 'pe':
                e = e.e
            ins.then_inc(self.semh[inc[0]], inc[1])


LAST_PE_LABELS = []


class _Count:
    def __init__(self, e, labels, tag):
        self.e, self.labels, self.tag = e, labels, tag

    def matmul(self, *a, **k):
        self.labels.append(self.tag)
        return self.e.matmul(*a, **k)

    def transpose(self, *a, **k):
        self.labels.append(self.tag)
        return self.e.transpose(*a, **k)


class Rot:
    def __init__(self, n):
        self.n = n
        self.i = 0
        self.held = set()

    def get(self):
        for _ in range(self.n):
            b = self.i
            self.i = (self.i + 1) % self.n
            if b not in self.held:
                return b
        raise RuntimeError("no free slot")


def build_nc():
    nc = bass.Bass("TRN2", target_bir_lowering=False)
    P = Prog(nc)

    def din(name, shape):
        return nc.dram_tensor(name, list(shape), F32, kind="ExternalInput").ap()

    def dout(name, shape):
        return nc.dram_tensor(name, list(shape), F32, kind="ExternalOutput").ap()

    xp = din("xp", [2048, D])
    xs = din("xs", [128, D])
    pp = din("pp", [2048, 256])
    psd = din("ps", [128, 256])
    stA = din("stA", [32, 512])
    stB = din("stB", [16, 30, 512])
    stF = din("stF", [32, DFF])
    w_in = din("w_in", [4, 128, KC * 640])
    w_out = din("w_out", [128, KC * D])
    w_up = din("w_up", [NJF, 128, KC * 256])
    w_dn = din("w_dn", [2, 8, 128, NJH * 128])
    w_pl = din("w_pl", [8, 128, 10 * 128])
    prm_d = din("prm", [128, NPRM])
    gfin_d = din("gfin", [1, D])

    y_p = dout("y_p", [2048, D])
    y_s = dout("y_s", [128, D])
    oa_p = dout("oa_p", [2, 512])
    ob_p = dout("ob_p", [30, 512])
    of_p = dout("of_p", [2, DFF])
    oa_s = dout("oa_s", [32, 512])
    ob_s = dout("ob_s", [16, 30, 512])
    of_s = dout("of_s", [32, DFF])

    sb = nc.alloc_sbuf_tensor
    hT = sb("hT", [128, KC, T], F32)
    uT = sb("uT", [128, KC, T], BF16)
    big = sb("big", [128, 16, T], BF16)
    big_addr = nc.lookup_mloc(big).addr
    ybp = nc.alloc_sbuf_tensor_at("ybp", [128, 4, T], F32, offset=big_addr + 8 * T * 2)
    xst = nc.alloc_sbuf_tensor_at("xst", [128, 8, 1024], F32, offset=big_addr)
    pT = sb("pT", [128, 2, T], BF16)
    regA = sb("regA", [128, 40 * 512], BF16)
    regB = sb("regB", [128, 32 * 512], BF16)
    xin = [sb("xin%d" % i, [128, 1024], F32) for i in range(2)]
    tbuf = [sb("tb%d" % i, [128, TBW], F32) for i in range(NTB)]
    hbuf = [sb("hb%d" % i, [128, TBW], BF16) for i in range(4)]
    hb_addr = nc.lookup_mloc(hbuf[0]).addr
    g_bc = nc.alloc_sbuf_tensor_at("g_bc", [128, D], F32, offset=hb_addr)
    GBK = [("hb", i) for i in range(4)]
    ssbuf = sb("ssbuf", [128, 16], F32)
    ssrot = Rot(4)
    prm = sb("prm_sb", [128, NPRM], F32)
    ident = sb("ident", [128, 128], F32)
    onesD = sb("onesD", [128, 128], BF16)
    onesL = sb("onesL", [128, 128], BF16)
    histA = sb("histA", [128, 4, 2], F32)
    histB = sb("histB", [128, 4, 30], F32)
    histF = sb("histF", [128, NJF, 2], F32)
    stA_T = sb("stA_T", [128, 4, 16], F32)
    stB_T = sb("stB_T", [128, 4, 240], F32)
    stF_T = sb("stF_T", [128, NJF, 16], F32)
    tlA = sb("tlA", [128, 4, 16], F32)
    tlB = sb("tlB", [128, 4, 64], F32)
    tlF = sb("tlF", [128, NJF, 16], F32)
    psb = [nc.alloc_psum_tensor("psb%d" % i, [128, 512], F32) for i in range(8)]

    pbank = Rot(8)
    trot = Rot(NTB)
    hrot = Rot(4)
    xrot = Rot(2)

    def TB():
        i = trot.get()
        return tbuf[i], ("tb", i)

    def HB():
        i = hrot.get()
        return hbuf[i], ("hb", i)

    def PB():
        b = pbank.get()
        return psb[b], ("ps", b)

    def A_units(lo_b, hi_b):
        return [("A", u) for u in range(lo_b // 1024, (hi_b + 1023) // 1024)]

    def B_units(lo_b, hi_b):
        return [("B", u) for u in range(lo_b // 1024, (hi_b + 1023) // 1024)]

    WIN_E = KC * 640
    WUP_E = KC * 256
    WPL_E = 10 * 128
    WDN_E = NJH * 128
    NWUP, NWDN, NWPL = 8, 4, 8

    def win_view(j):
        return regA[:, j * WIN_E:(j + 1) * WIN_E].rearrange("p (k c) -> p k c", c=640)

    def win_keys(j):
        return A_units(j * WIN_E * 2, (j + 1) * WIN_E * 2)

    def wup_view(sl):
        return regA[:, sl * WUP_E:(sl + 1) * WUP_E].rearrange("p (k c) -> p k c", c=256)

    def wup_keys(sl):
        return A_units(sl * WUP_E * 2, (sl + 1) * WUP_E * 2)

    def wpl_view(sl):
        o = sl * WPL_E
        return regB[:, o:o + WPL_E].rearrange("p (k c) -> p k c", c=128)

    def wpl_keys(sl):
        o = sl * WPL_E * 2
        return B_units(o, o + WPL_E * 2)

    wout_view = regB[:, 0:KC * D].rearrange("p (k c) -> p k c", c=D)
    wout_keys = B_units(0, KC * D * 2)

    WDN_BASE = 20 * 512

    def wdn_view(sl):
        o = WDN_BASE + sl * 1536
        return regB[:, o:o + WDN_E].rearrange("p (k c) -> p k c", c=128)

    def wdn_keys(sl):
        o = (WDN_BASE + sl * 1536) * 2
        return B_units(o, o + WDN_E * 2)

    def diag_off(j):
        return j * NPE * 128 if j < 2 else 16 * 512 + (j - 2) * NPE * 128

    def diag_view(j, k):
        o = diag_off(j) + k * 128
        return regB[:, o:o + 128]

    def diag_keys(j):
        return B_units(diag_off(j) * 2, (diag_off(j) + NPE * 128) * 2)

    def pcol(c):
        return prm[:, c:c + 1]

    P.op('pool', lambda e: e.memset(ident[:, :], 0.0), writes=[("ident",)])
    P.op('pool', lambda e: e.affine_select(out=ident[:, :], in_=ident[:, :], pattern=[[-1, 128]],
                                           compare_op=ALU.not_equal, fill=1.0, base=0,
                                           channel_multiplier=1),
         reads=[("ident",)], writes=[("ident",)])
    P.op('pool', lambda e: e.memset(onesD[:, :], 1.0 / D), writes=[("onesD",)])
    P.op('pool', lambda e: e.memset(onesL[:, :], 1.0 / 512), writes=[("onesL",)])
    P.op('sp', lambda e: e.dma_start(out=prm[:, :], in_=prm_d), writes=[("prm",)], dma=("prm",))

    def load_rows_T(src_rows, R, C, dst_fn, dst_keys, evac_eng_cycle):
        sl = xrot.get()
        xk = ("xin", sl)
        xt = xin[sl]
        P.op('sp', lambda e: e.dma_start(out=xt[0:R, 0:C], in_=src_rows), writes=[xk], dma=xk)
        nblk = C // 128
        for g0 in range(0, nblk, 4):
            nb = min(4, nblk - g0)
            pt, pk = PB()

            def tr(e, g0=g0, nb=nb, pt=pt):
                ins = None
                for i in range(nb):
                    ins = e.transpose(out=pt[:, i * 128:i * 128 + R],
                                      in_=xt[0:R, (g0 + i) * 128:(g0 + i + 1) * 128],
                                      identity=ident[0:R, 0:R])
                return ins
            P.op('pe', tr, reads=[xk, ("ident",)], writes=[pk])
            src = pt[:, 0:nb * 128].rearrange("p (a b) -> p a b", b=128)[:, :, 0:R]
            eng = evac_eng_cycle[(g0 // 4) % len(evac_eng_cycle)]
            dst = dst_fn(g0, nb)
            if eng == 'act':
                P.op('act', lambda e, dst=dst, src=src: e.activation(out=dst, in_=src, func=AF.Copy),
                     reads=[pk], writes=dst_keys)
            else:
                P.op('dve', lambda e, dst=dst, src=src: e.tensor_copy(out=dst, in_=src),
                     reads=[pk], writes=dst_keys)

    def store_rows_T(src_fn, src_keys, R, C, dst_rows_list):
        sl = xrot.get()
        xk = ("xin", sl)
        xt = xin[sl]
        nblk = C // 128
        for g0 in range(0, nblk, 4):
            nb = min(4, nblk - g0)
            pt, pk = PB()

            def tr(e, g0=g0, nb=nb, pt=pt):
                ins = None
                for i in range(nb):
                    ins = e.transpose(out=pt[0:R, i * 128:(i + 1) * 128], in_=src_fn(g0 + i),
                                      identity=ident[:, :])
                return ins
            P.op('pe', tr, reads=list(src_keys) + [("ident",)], writes=[pk])
            P.op('act', lambda e, g0=g0, nb=nb, pt=pt: e.activation(
                out=xt[0:R, g0 * 128:(g0 + nb) * 128], in_=pt[0:R, 0:nb * 128], func=AF.Copy),
                reads=[pk], writes=[xk])
        for dram_ap, r0, r1 in dst_rows_list:
            P.op('act', lambda e, dram_ap=dram_ap, r0=r0, r1=r1: e.dma_start(out=dram_ap, in_=xt[r0:r1, 0:C]),
                 reads=[xk], dma=xk)

    def norm_a(seg):
        n, off = seg.n, seg.off
        hk = [("hT", m, seg.idx) for m in range(KC)]
        sqk = [("uT", kc, seg.idx) for kc in range(KC)]
        P.op('act', lambda e: e.activation(out=uT[:, 0:KC, off:off + n], in_=hT[:, 0:KC, off:off + n],
                                           func=AF.Square), reads=hk, writes=sqk)

    def norm(seg, gbase, to_hT=False):
        norm_a(seg)
        norm_b(seg, gbase, to_hT)

    def norm_b(seg, gbase, to_hT=False):
        n, off = seg.n, seg.off
        sqk = [("uT", kc, seg.idx) for kc in range(KC)]
        pt, pk = PB()

        def mm(e):
            ins = None
            for kc in range(KC):
                ins = e.matmul(pt[:, 0:n], onesD[:, :], uT[:, kc, off:off + n],
                               start=(kc == 0), stop=(kc == KC - 1))
            return ins
        P.op('pe', mm, reads=sqk + [("onesD",)], writes=[pk])
        sd, sdk = TB()
        P.op('act', lambda e: e.activation(out=sd[:, 0:n], in_=pt[:, 0:n], func=AF.Ln, bias=EPS),
             reads=[pk], writes=[sdk])
        rs, rsk = TB()
        P.op('act', lambda e: e.activation(out=rs[:, 0:n], in_=sd[:, 0:n], func=AF.Exp, scale=-0.5),
             reads=[sdk], writes=[rsk])
        for kc in range(KC):
            if to_hT:
                dst, dk = hT[:, kc, off:off + n], ("hT", kc, seg.idx)
            else:
                dst, dk = uT[:, kc, off:off + n], ("uT", kc, seg.idx)
            P.op('dve', lambda e, kc=kc, dst=dst: e.scalar_tensor_tensor(
                out=dst, in0=hT[:, kc, off:off + n], scalar=pcol(gbase + kc), in1=rs[:, 0:n],
                op0=ALU.mult, op1=ALU.mult),
                reads=[("hT", kc, seg.idx), rsk, ("prm",)], writes=[dk])

    def set_head(seg, buf, bk, H, q, hist_ap, hist_key, st_ap, st_key):
        hd = seg.head(buf, H)
        if seg.kind == 's':
            P.op('pool', lambda e: e.tensor_copy(out=hd, in_=st_ap), reads=[st_key], writes=[bk])
        elif q == 0 and seg.idx == 0:
            P.op('pool', lambda e: e.memset(hd, 0.0), writes=[bk])
        else:
            P.op('pool', lambda e: e.tensor_copy(out=hd, in_=hist_ap), reads=[hist_key], writes=[bk])

    def conv_taps(seg, buf, bk, H, K, wbase, bias_col, acc, acck, in_place_keys=None, first_on_act=True):
        if first_on_act:
            if bias_col is None:
                P.op('act', lambda e: e.activation(out=acc, in_=seg.tap(buf, H, 0), func=AF.Copy, scale=pcol(wbase)),
                     reads=[bk, ("prm",)], writes=acck)
            else:
                P.op('act', lambda e: e.activation(out=acc, in_=seg.tap(buf, H, 0), func=AF.Identity,
                                                   scale=pcol(wbase), bias=pcol(bias_col)),
                     reads=[bk, ("prm",)], writes=acck)
        elif bias_col is None:
            P.op('dve', lambda e: e.tensor_scalar(out=acc, in0=seg.tap(buf, H, 0), scalar1=pcol(wbase),
                                                  scalar2=None, op0=ALU.mult),
                 reads=[bk, ("prm",)], writes=acck)
        else:
            P.op('dve', lambda e: e.tensor_scalar(out=acc, in0=seg.tap(buf, H, 0), scalar1=pcol(wbase),
                                                  scalar2=pcol(bias_col), op0=ALU.mult, op1=ALU.add),
                 reads=[bk, ("prm",)], writes=acck)
        for k in range(1, K):
            P.op('dve', lambda e, k=k: e.scalar_tensor_tensor(
                out=acc, in0=seg.tap(buf, H, k), scalar=pcol(wbase + k), in1=acc,
                op0=ALU.mult, op1=ALU.add),
                reads=[bk, ("prm",)] + list(acck), writes=acck)

    def issue_win(j):
        P.op('pool', lambda e: e.dma_start(out=win_view(j), in_=w_in[j].rearrange("p (k c) -> p k c", c=640)),
             writes=win_keys(j), dma=("win", j))

    def build_diag(j, KSTEP=4):
        for k0 in range(0, NPE, KSTEP):
            nk = min(KSTEP, NPE - k0)
            o0 = diag_off(j) + k0 * 128
            out = regB[:, o0:o0 + nk * 128].rearrange("p (k c) -> p k c", c=128)
            i3 = ident[:, :].rearrange("p (o c) -> p o c", o=1).broadcast_to([128, nk, 128])
            w3 = prm[:, P_WB + 31 * j + k0:P_WB + 31 * j + k0 + nk].rearrange(
                "p (k o) -> p k o", o=1).broadcast_to([128, nk, 128])
            P.op('pool', lambda e, out=out, i3=i3, w3=w3: e.tensor_tensor(out=out, in0=i3, in1=w3, op=ALU.mult),
                 reads=[("ident",), ("prm",)], writes=diag_keys(j))

    def issue_wout():
        P.op('pool', lambda e: e.dma_start(out=wout_view, in_=w_out.rearrange("p (k c) -> p k c", c=D)),
             writes=wout_keys, dma=("wout",))

    def issue_wup(i):
        sl = i % NWUP
        P.op('pool', lambda e: e.dma_start(out=wup_view(sl), in_=w_up[i].rearrange("p (k c) -> p k c", c=256)),
             writes=wup_keys(sl), dma=("wup", sl))

    def issue_wdn(i):
        sl = i % NWDN
        h, m = divmod(i, 8)
        P.op('pool', lambda e: e.dma_start(out=wdn_view(sl), in_=w_dn[h, m].rearrange("p (k c) -> p k c", c=128)),
             writes=wdn_keys(sl), dma=("wdn", sl))

    def issue_wpl(m):
        sl = m % NWPL
        P.op('pool', lambda e: e.dma_start(out=wpl_view(sl), in_=w_pl[m].rearrange("p (k c) -> p k c", c=128)),
             writes=wpl_keys(sl), dma=("wpl", sl))

    def tile_rows(q, seg, t):
        R = 128 if seg.kind == 'p' else 64
        to = seg.off + t * 128
        if seg.kind == 'p':
            r0 = q * 1024 + to
            return R, to, xp[r0:r0 + R, :], pp[r0:r0 + R, :]
        return R, to, xs[q * 64:q * 64 + 64, :], psd[q * 64:q * 64 + 64, :]

    def load_tile(q, seg, t):
        R, to, xrows, prows = tile_rows(q, seg, t)
        load_rows_T(xrows, R, D, lambda g0, nb: hT[:, g0:g0 + nb, to:to + R],
                    [("hT", m, seg.idx) for m in range(KC)], ['act', 'dve'])

    def xst_keys(seg):
        blks = range(0, 8) if seg.idx == 0 else range(7, 16)
        return [("big", b, s_) for b in blks for s_ in range(3)]

    def issue_xload(q, seg):
        r0 = q * 1024 + seg.off
        src = xp[r0:r0 + 512, :].rearrange("(t p) d -> p t d", p=128)
        P.op('sp', lambda e: e.dma_start(out=xst[:, seg.idx * 4:(seg.idx + 1) * 4, :], in_=src),
             writes=xst_keys(seg), dma=("xst", seg.idx))

    def head_tile(q, seg, t):
        if seg.kind == 's':
            load_tile(q, seg, t)
            return
        to = seg.off + t * 128
        ti = seg.idx * 4 + t
        hk = [("hT", m, seg.idx) for m in range(KC)]
        for g0 in (0, 4):
            pt, pk = PB()

            def tr(e, g0=g0, pt=pt):
                ins = None
                for i in range(4):
                    ins = e.transpose(out=pt[:, i * 128:(i + 1) * 128],
                                      in_=xst[:, ti, (g0 + i) * 128:(g0 + i + 1) * 128], identity=ident[:, :])
                return ins
            P.op('pe', tr, reads=xst_keys(seg) + [("ident",)], writes=[pk])
            src = pt[:, 0:512].rearrange("p (a b) -> p a b", b=128)
            dst = hT[:, g0:g0 + 4, to:to + 128]
            if g0 == 0:
                P.op('act', lambda e, dst=dst, src=src: e.activation(out=dst, in_=src, func=AF.Copy),
                     reads=[pk], writes=hk)
            else:
                P.op('dve', lambda e, dst=dst, src=src: e.tensor_copy(out=dst, in_=src), reads=[pk], writes=hk)

    def load_p_tile(q, seg, t):
        R, to, xrows, prows = tile_rows(q, seg, t)
        sl = xrot.get()
        pb_, pbk = xin[sl], ("xin", sl)
        P.op('sp', lambda e: e.dma_start(out=pb_[0:R, 0:256], in_=prows), writes=[pbk], dma=pbk)
        pt, pk = PB()

        def trp(e):
            ins = None
            for i in range(2):
                ins = e.transpose(out=pt[:, i * 128:i * 128 + R], in_=pb_[0:R, i * 128:(i + 1) * 128],
                                  identity=ident[0:R, 0:R])
            return ins
        P.op('pe', trp, reads=[pbk, ("ident",)], writes=[pk])
        P.op('act', lambda e: e.activation(
            out=pT[:, 0:2, to:to + R],
            in_=pt[:, 0:256].rearrange("p (a b) -> p a b", b=128)[:, :, 0:R], func=AF.Copy),
            reads=[pk], writes=[("pT", seg.idx)])

    def state_tasks(q):
        tasks = []
        tasks.append(lambda: load_rows_T(stA[q * 16:q * 16 + 16, :], 16, 512,
                                         lambda g0, nb: stA_T[:, g0:g0 + nb, 0:16], [("stA",)], ['act']))
        stB_rows = stB[q * 8:q * 8 + 8, :, :].rearrange("s r c -> (s r) c")

        def lb(r0, R):
            load_rows_T(stB_rows[r0:r0 + R, :], R, 512,
                        lambda g0, nb: stB_T[:, g0:g0 + nb, r0:r0 + R], [("stB",)], ['act'])
        tasks.append(lambda: lb(0, 128))
        tasks.append(lambda: lb(128, 112))

        def lf(c0, C):
            load_rows_T(stF[q * 16:q * 16 + 16, c0:c0 + C], 16, C,
                        lambda g0, nb: stF_T[:, c0 // 128 + g0:c0 // 128 + g0 + nb, 0:16], [("stF",)], ['act'])
        tasks.append(lambda: P.op('sp', lambda e: e.dma_start(out=ob_s[q * 8:q * 8 + 8, 0:22, :],
                                                            in_=stB[q * 8:q * 8 + 8, 8:30, :]), dma=("obcopy",)))
        for c0 in range(0, DFF, 1024):
            tasks.append(lambda c0=c0: lf(c0, min(1024, DFF - c0)))
        return [("states.q%d" % q, t) for t in tasks]

    BG = []

    def bg_run(k=1):
        for _ in range(k):
            if BG:
                tag = P.tag
                P.tag = "bg"
                BG.pop(0)[1]()
                P.tag = tag

    def bg_flush_through(name):
        while any(nm == name for nm, _ in BG):
            bg_run(1)

    def zgroup(seg, j, tcol):
        n, off, si = seg.n, seg.off, seg.idx
        uk = [("uT", kc, si) for kc in range(KC)]
        wv = win_view(j)
        pt, pk = PB()

        def mm(e):
            ins = None
            for kc in range(KC):
                ins = e.matmul(pt[:, 0:n], wv[:, kc, tcol * 128:(tcol + 1) * 128],
                               uT[:, kc, off:off + n], start=(kc == 0), stop=(kc == KC - 1))
            return ins
        P.op('pe', mm, reads=uk + win_keys(j), writes=[pk])
        return pt, pk

    def mix_zA(q, seg, j):
        n, off, si = seg.n, seg.off, seg.idx
        pv, pvk = zgroup(seg, j, 2)
        pc, pck = zgroup(seg, j, 1)
        pbb, pbk = zgroup(seg, j, 0)
        vs, vsk = TB()
        P.op('act', lambda e: e.activation(out=vs[:, 0:n], in_=pv[:, 0:n], func=AF.Copy), reads=[pvk], writes=[vsk])
        bs, bsk = TB()
        P.op('act', lambda e: e.activation(out=bs[:, 0:n], in_=pbb[:, 0:n], func=AF.Copy), reads=[pbk], writes=[bsk])
        cvh, cvk = TB()
        P.op('dve', lambda e: e.tensor_tensor(out=seg.body(cvh, 2), in0=seg.cv(pc[:, 0:n]), in1=seg.cv(vs[:, 0:n]),
                                              op=ALU.mult), reads=[pck, vsk], writes=[cvk])
        set_head(seg, cvh, cvk, 2, q, histA[:, j, :], ("histA", j),
                 stA_T[:, j, :].rearrange("p (s r) -> p s r", r=2), ("stA",))
        acc, ack = TB()
        conv_taps(seg, cvh, cvk, 2, 3, P_WA + 3 * j, None, seg.cv(acc[:, 0:n]), [ack])
        P.op('dve', lambda e: e.tensor_tensor(out=seg.cv(big[:, j, off:off + n]), in0=seg.cv(acc[:, 0:n]),
                                              in1=seg.cv(bs[:, 0:n]), op=ALU.mult),
             reads=[ack, bsk], writes=[("big", j, si)])
        if seg.kind == 'p':
            P.op('pool', lambda e: e.tensor_copy(out=histA[:, j, :], in_=seg.tail(cvh, 2)),
                 reads=[cvk], writes=[("histA", j)])
        else:
            P.op('pool', lambda e: e.tensor_copy(out=tlA[:, j, :].rearrange("p (s r) -> p s r", r=2),
                                                 in_=seg.tail(cvh, 2)), reads=[cvk], writes=[("tlA",)])

    def mix_zB(q, seg, j):
        n, off, si = seg.n, seg.off, seg.idx
        pg, pgk = zgroup(seg, j, 4)
        pa, pak = zgroup(seg, j, 3)
        sg, sgk = TB()
        P.op('act', lambda e: e.activation(out=sg[:, 0:n], in_=pg[:, 0:n], func=AF.Sigmoid), reads=[pgk], writes=[sgk])
        glh, glk = TB()
        P.op('dve', lambda e: e.tensor_tensor(out=seg.body(glh, 30), in0=seg.cv(pa[:, 0:n]), in1=seg.cv(sg[:, 0:n]),
                                              op=ALU.mult), reads=[pak, sgk], writes=[glk])
        set_head(seg, glh, glk, 30, q, histB[:, j, :], ("histB", j),
                 stB_T[:, j, :].rearrange("p (s r) -> p s r", r=30), ("stB",))
        if seg.kind == 'p':
            P.op('pool', lambda e: e.tensor_copy(out=histB[:, j, :], in_=seg.tail(glh, 30)),
                 reads=[glk], writes=[("histB", j)])
        else:
            P.op('pool', lambda e: e.tensor_copy(out=tlB[:, j, :].rearrange("p (s t) -> p s t", t=8),
                                                 in_=seg.body(glh, 30)), reads=[glk], writes=[("tlB",)])
        W = (30 + n) if seg.kind == 'p' else 8 * 38
        g16, g16k = HB()
        P.op('act', lambda e: e.activation(out=g16[:, 0:W], in_=glh[:, 0:W], func=AF.Copy), reads=[glk], writes=[g16k])
        return g16, g16k

    def mix_conv(seg, j, g16, g16k, mean_t, mean_k, ex2_t, ex2_k):
        n, off, si = seg.n, seg.off, seg.idx
        ybk = [("big", 8 + 2 * j + d, s2) for d in range(2) for s2 in range(3)]
        pcv, pcvk = PB()

        def mmc(e):
            ins = None
            for k in range(NPE):
                ins = e.matmul(seg.cv(pcv[:, 0:n]), diag_view(j, k), seg.tap(g16, 30, k),
                               start=(k == 0), stop=(k == NPE - 1))
            return ins
        P.op('pe', mmc, reads=[g16k] + diag_keys(j), writes=[pcvk])
        P.op('act', lambda e: e.activation(out=ybp[:, j, off:off + n], in_=pcv[:, 0:n], func=AF.Identity,
                                           bias=pcol(P_CBB + j)), reads=[pcvk, ("prm",)], writes=ybk)
        y16, y16k = HB()
        P.op('act', lambda e: e.activation(out=y16[:, 0:n], in_=pcv[:, 0:n], func=AF.Identity,
                                           bias=pcol(P_CBB + j)), reads=[pcvk, ("prm",)], writes=[y16k])
        s16, s16k = HB()
        P.op('act', lambda e: e.activation(out=s16[:, 0:n], in_=pcv[:, 0:n], func=AF.Square,
                                           bias=pcol(P_CBB + j)), reads=[pcvk, ("prm",)], writes=[s16k])
        def stats():
            P.op('pe', lambda e: e.matmul(mean_t[:, 0:n], onesL[:, :], y16[:, 0:n], start=(j == 0), stop=(j == 3),
                                          skip_group_check=True), reads=[y16k, ("onesL",)], writes=[mean_k])
            P.op('pe', lambda e: e.matmul(ex2_t[:, 0:n], onesL[:, :], s16[:, 0:n], start=(j == 0), stop=(j == 3),
                                          skip_group_check=True), reads=[s16k, ("onesL",)], writes=[ex2_k])
        return stats

    def ln_swish(seg, j, rsl, rslk, nmr, nmrk):
        n, off, si = seg.n, seg.off, seg.idx
        ybk = [("big", 8 + 2 * j + d, s2) for d in range(2) for s2 in range(3)]
        t1, t1k = TB()
        P.op('dve', lambda e: e.tensor_tensor(out=t1[:, 0:n], in0=ybp[:, j, off:off + n], in1=rsl[:, 0:n], op=ALU.mult),
             reads=ybk + [rslk], writes=[t1k])
        P.op('dve', lambda e: e.tensor_tensor(out=t1[:, 0:n], in0=t1[:, 0:n], in1=nmr[:, 0:n], op=ALU.add),
             reads=[t1k, nmrk], writes=[t1k])
        P.op('act', lambda e: e.activation(out=big[:, 4 + j, off:off + n], in_=t1[:, 0:n], func=AF.Silu,
                                           scale=pcol(P_LNG + j), bias=pcol(P_LNB + j)),
             reads=[t1k, ("prm",)], writes=[("big", 4 + j, si)])

    LNQ = []

    def ln_finalize(seg, mean_t, mean_k, ex2_t, ex2_k):
        n = seg.n
        st = {}

        def part0():
            msq, msqk = TB()
            P.op('act', lambda e: e.activation(out=msq[:, 0:n], in_=mean_t[:, 0:n], func=AF.Square),
                 reads=[mean_k], writes=[msqk])
            var, vark = TB()
            P.op('dve', lambda e: e.tensor_tensor(out=var[:, 0:n], in0=ex2_t[:, 0:n], in1=msq[:, 0:n], op=ALU.subtract),
                 reads=[ex2_k, msqk], writes=[vark])
            P.op('dve', lambda e: e.tensor_scalar(out=var[:, 0:n], in0=var[:, 0:n], scalar1=0.0, scalar2=None,
                                                  op0=ALU.max), reads=[vark], writes=[vark])
            sdl, sdlk = TB()
            P.op('act', lambda e: e.activation(out=sdl[:, 0:n], in_=var[:, 0:n], func=AF.Ln, bias=EPS),
                 reads=[vark], writes=[sdlk])
            rsl, rslk = TB()
            trot.held.add(rslk[1])
            P.op('act', lambda e: e.activation(out=rsl[:, 0:n], in_=sdl[:, 0:n], func=AF.Exp, scale=-0.5),
                 reads=[sdlk], writes=[rslk])
            nmr, nmrk = TB()
            trot.held.add(nmrk[1])
            P.op('dve', lambda e: e.scalar_tensor_tensor(out=nmr[:, 0:n], in0=mean_t[:, 0:n], scalar=-1.0,
                                                         in1=rsl[:, 0:n], op0=ALU.mult, op1=ALU.mult),
                 reads=[mean_k, rslk], writes=[nmrk])
            pbank.held.discard(mean_k[1])
            pbank.held.discard(ex2_k[1])
            st['v'] = (rsl, rslk, nmr, nmrk)

        def partj(j):
            rsl, rslk, nmr, nmrk = st['v']
            ln_swish(seg, j, rsl, rslk, nmr, nmrk)
            if j == 3:
                trot.held.discard(rslk[1])
                trot.held.discard(nmrk[1])
        part0()
        LNQ.append(lambda: (partj(0), partj(1)))
        LNQ.append(lambda: (partj(2), partj(3)))

    def lnq_run(k=1):
        for _ in range(k):
            if LNQ:
                LNQ.pop(0)()

    def mixer_phase(q, order, hook=None, bg_first=2):
        units = [(seg, j) for seg in order for j in range(4)]
        stat = {}

        def do_conv(pv):
            seg, j, g16, g16k = pv
            if j == 0:
                mean_t, mean_k = PB()
                pbank.held.add(mean_k[1])
                ex2_t, ex2_k = PB()
                pbank.held.add(ex2_k[1])
                stat[seg.idx] = (mean_t, mean_k, ex2_t, ex2_k)
            P.tag = "q%d.mix.s%d" % (q, seg.idx)
            pend_stats.append(mix_conv(seg, j, g16, g16k, *stat[seg.idx]))
            if hook is not None:
                hook(seg, j)
            if j == 3:
                pend_ln.append(seg)

        pend_stats = []
        pend_ln = []

        def flush_pending():
            for f in pend_stats:
                f()
            del pend_stats[:]
            for sg_ in pend_ln:
                ln_finalize(sg_, *stat[sg_.idx])
            del pend_ln[:]

        prev = None
        for ui, (seg, j) in enumerate(units):
            if j == 0:
                bg_flush_through("head.q%d.s%d" % (q, seg.idx))
                if seg.kind == 's':
                    bg_flush_through("ststore.q%d" % (q - 1))
                    bg_flush_through("states.q%d" % q)
            P.tag = "q%d.mix.s%d" % (q, seg.idx)
            g16, g16k = mix_zB(q, seg, j)
            if prev is not None:
                do_conv(prev)
            P.tag = "q%d.mix.s%d" % (q, seg.idx)
            mix_zA(q, seg, j)
            prev = (seg, j, g16, g16k)
            flush_pending()
            lnq_run(1)
            bg_run(bg_first if ui < 4 else 1)
        do_conv(prev)
        flush_pending()
        lnq_run(len(LNQ))

    def resid_add(seg, m, pt, pk):
        n, off, si = seg.n, seg.off, seg.idx
        P.op('dve', lambda e: e.tensor_tensor(out=hT[:, m, off:off + n], in0=hT[:, m, off:off + n],
                                              in1=pt[:, 0:n], op=ALU.add),
             reads=[pk, ("hT", m, si)], writes=[("hT", m, si)])

    def outproj(seg, m):
        n, off, si = seg.n, seg.off, seg.idx
        yk = [("big", b, si) for b in range(8)]
        pt, pk = PB()

        def mm(e):
            ins = None
            for kc in range(KC):
                ins = e.matmul(pt[:, 0:n], wout_view[:, kc, m * 128:(m + 1) * 128],
                               big[:, kc, off:off + n], start=(kc == 0), stop=(kc == KC - 1))
            return ins
        P.op('pe', mm, reads=yk + wout_keys, writes=[pk])
        resid_add(seg, m, pt, pk)

    def ffn_up(q, j, jj, seg):
        n, off, si = seg.n, seg.off, seg.idx
        sl = j % NWUP
        wv = wup_view(sl)
        uk = [("uT", kc, si) for kc in range(KC)]
        gt, gk = PB()
        vt, vk = PB()

        def grp(pt, pk, c0):
            def mm(e):
                ins = None
                for kc in range(KC):
                    ins = e.matmul(pt[:, 0:n], wv[:, kc, c0:c0 + 128], uT[:, kc, off:off + n],
                                   start=(kc == 0), stop=(kc == KC - 1))
                return ins
            P.op('pe', mm, reads=uk + wup_keys(sl), writes=[pk])
        grp(gt, gk, 0)
        grp(vt, vk, 128)
        gh, ghk = TB()
        P.op('act', lambda e: e.activation(out=seg.body(gh, 2), in_=seg.cv(gt[:, 0:n]), func=AF.Copy),
             reads=[gk], writes=[ghk])
        vb, vbk = TB()
        P.op('act', lambda e: e.activation(out=vb[:, 0:n], in_=vt[:, 0:n], func=AF.Copy), reads=[vk], writes=[vbk])
        set_head(seg, gh, ghk, 2, q, histF[:, j, :], ("histF", j),
                 stF_T[:, j, :].rearrange("p (s r) -> p s r", r=2), ("stF",))
        acc, ack = TB()
        conv_taps(seg, gh, ghk, 2, 3, P_WF + 3 * j, P_CBF + j, seg.cv(acc[:, 0:n]), [ack], first_on_act=True)
        if seg.kind == 'p':
            P.op('pool', lambda e: e.tensor_copy(out=histF[:, j, :], in_=seg.tail(gh, 2)),
                 reads=[ghk], writes=[("histF", j)])
        else:
            P.op('pool', lambda e: e.tensor_copy(out=tlF[:, j, :].rearrange("p (s r) -> p s r", r=2),
                                                 in_=seg.tail(gh, 2)), reads=[ghk], writes=[("tlF",)])
        return (seg, jj, acc, ack, vb, vbk)

    def ffn_up2(ctx):
        seg, jj, acc, ack, vb, vbk = ctx
        n, off, si = seg.n, seg.off, seg.idx
        sg, sgk = TB()
        P.op('act', lambda e: e.activation(out=sg[:, 0:n], in_=acc[:, 0:n], func=AF.Silu), reads=[ack], writes=[sgk])
        P.op('dve', lambda e: e.tensor_tensor(out=big[:, jj, off:off + n], in0=sg[:, 0:n], in1=vb[:, 0:n], op=ALU.mult),
             reads=[sgk, vbk], writes=[("big", jj, si)])

    def ffn_down(i, m, seg):
        n, off, si = seg.n, seg.off, seg.idx
        sl = i % NWDN
        dv = wdn_view(sl)
        fk = [("big", b, si) for b in range(NJH)]
        pt, pk = PB()

        def mm(e):
            ins = None
            for jj in range(NJH):
                ins = e.matmul(pt[:, 0:n], dv[:, jj, :], big[:, jj, off:off + n],
                               start=(jj == 0), stop=(jj == NJH - 1))
            return ins
        P.op('pe', mm, reads=fk + wdn_keys(sl), writes=[pk])
        resid_add(seg, m, pt, pk)

    def ple(m, seg):
        n, off, si = seg.n, seg.off, seg.idx
        sl = m % NWPL
        pv = wpl_view(sl)
        uk = [("uT", kc, si) for kc in range(KC)]
        gt, gk = PB()
        et, ek = PB()

        def mmg(e):
            ins = None
            for kc in range(KC):
                ins = e.matmul(gt[:, 0:n], pv[:, kc, :], uT[:, kc, off:off + n], start=(kc == 0), stop=(kc == KC - 1))
            return ins
        P.op('pe', mmg, reads=uk + wpl_keys(sl), writes=[gk])

        def mme(e):
            ins = None
            for kc in range(2):
                ins = e.matmul(et[:, 0:n], pv[:, 8 + kc, :], pT[:, kc, off:off + n], start=(kc == 0), stop=(kc == 1))
            return ins
        P.op('pe', mme, reads=[("pT", si)] + wpl_keys(sl), writes=[ek])
        sg, sgk = TB()
        P.op('act', lambda e: e.activation(out=sg[:, 0:n], in_=gt[:, 0:n], func=AF.Sigmoid), reads=[gk], writes=[sgk])
        t2, t2k = TB()
        P.op('dve', lambda e: e.tensor_tensor(out=t2[:, 0:n], in0=et[:, 0:n], in1=sg[:, 0:n], op=ALU.mult),
             reads=[ek, sgk], writes=[t2k])
        P.op('dve', lambda e: e.tensor_tensor(out=hT[:, m, off:off + n], in0=hT[:, m, off:off + n],
                                              in1=t2[:, 0:n], op=ALU.add),
             reads=[t2k, ("hT", m, si)], writes=[("hT", m, si)])

    def store_tile(q, seg, t):
        R = 128 if seg.kind == 'p' else 64
        to = seg.off + t * 128
        if seg.kind == 'p':
            r0 = q * 1024 + to
            dst = y_p[r0:r0 + R, :]
        else:
            dst = y_s[q * 64:q * 64 + 64, :]
        store_rows_T(lambda i: hT[:, i, to:to + R], [("hT", m, seg.idx) for m in range(KC)], R, D, [(dst, 0, R)])

    def load_gbc():
        P.op('sp', lambda e: e.dma_start(out=g_bc[:, :], in_=gfin_d.broadcast_to([128, D])), writes=GBK, dma=("gbc",))

    RSTD = {}

    def tail_stats(q, seg):
        P.tag = "q%d.tail.s%d" % (q, seg.idx)
        nt = 4 if seg.kind == 'p' else 1
        R = 128 if seg.kind == 'p' else 64
        gi = ssrot.get()
        c0 = 4 * gi
        ssk = ("ss", gi)
        sqk = [("uT", kc, seg.idx) for kc in range(KC)]
        pt, pk = PB()

        def mm(e):
            ins = None
            for t in range(nt):
                to = seg.off + t * 128
                for kc in range(KC):
                    ins = e.matmul(pt[0:R, t:t + 1], uT[:, kc, to:to + R], onesD[:, 0:1],
                                   start=(kc == 0), stop=(kc == KC - 1), skip_group_check=True)
            return ins
        P.op('pe', mm, reads=sqk + [("onesD",)], writes=[pk])
        lnb, lnk = TB()
        P.op('act', lambda e: e.activation(out=lnb[0:R, 0:nt], in_=pt[0:R, 0:nt], func=AF.Ln, bias=EPS),
             reads=[pk], writes=[lnk])
        P.op('act', lambda e: e.activation(out=ssbuf[0:R, c0:c0 + nt], in_=lnb[0:R, 0:nt], func=AF.Exp, scale=-0.5),
             reads=[lnk], writes=[ssk])
        RSTD[seg.idx] = (c0, ssk)

    def store_tile_norm(q, seg, t):
        P.tag = "q%d.tail.s%d" % (q, seg.idx)
        R = 128 if seg.kind == 'p' else 64
        to = seg.off + t * 128
        if seg.kind == 'p':
            r0 = q * 1024 + to
            dst = y_p[r0:r0 + R, :]
        else:
            dst = y_s[q * 64:q * 64 + 64, :]
        c0, ssk = RSTD[seg.idx]
        sl = xrot.get()
        xk = ("xin", sl)
        xt = xin[sl]
        hk = [("hT", m, seg.idx) for m in range(KC)]
        for h_, g0 in enumerate((0, 4)):
            pt, pk = PB()

            def tr(e, g0=g0, pt=pt):
                ins = None
                for i in range(4):
                    ins = e.transpose(out=pt[0:R, i * 128:(i + 1) * 128], in_=hT[:, g0 + i, to:to + R],
                                      identity=ident[:, :])
                return ins
            P.op('pe', tr, reads=hk + [("ident",)], writes=[pk])
            P.op('dve', lambda e, pt=pt, h_=h_: e.scalar_tensor_tensor(
                out=xt[0:R, h_ * 512:(h_ + 1) * 512], in0=pt[0:R, 0:512], scalar=ssbuf[0:R, c0 + t:c0 + t + 1],
                in1=g_bc[0:R, h_ * 512:(h_ + 1) * 512], op0=ALU.mult, op1=ALU.mult),
                reads=[pk, ssk] + GBK, writes=[xk])
        P.op('sp', lambda e: e.dma_start(out=dst, in_=xt[0:R, 0:D]), reads=[xk], dma=xk)

    def state_store_tasks(q):
        ts = []
        ts.append(lambda: store_rows_T(lambda i: tlA[:, i, :], [("tlA",)], 16, 512,
                                       [(oa_s[q * 16:q * 16 + 16, :], 0, 16)]))
        ts.append(lambda: store_rows_T(lambda i: tlB[:, i, :], [("tlB",)], 64, 512,
                                       [(ob_s[q * 8 + s_, 22:30, :], s_ * 8, s_ * 8 + 8) for s_ in range(8)]))

        def sf(c0, C):
            store_rows_T(lambda i: tlF[:, c0 // 128 + i, :], [("tlF",)], 16, C,
                         [(of_s[q * 16:q * 16 + 16, c0:c0 + C], 0, 16)])
        for c0 in range(0, DFF, 1024):
            ts.append(lambda c0=c0: sf(c0, min(1024, DFF - c0)))
        if q == NPASS - 1:
            ts.append(lambda: store_rows_T(lambda i: histA[:, i, :], [("histA", j) for j in range(4)], 2, 512,
                                           [(oa_p, 0, 2)]))
            ts.append(lambda: store_rows_T(lambda i: histB[:, i, :], [("histB", j) for j in range(4)], 30, 512,
                                           [(ob_p, 0, 30)]))

            def sp_(c0, C):
                store_rows_T(lambda i: histF[:, c0 // 128 + i, :], [("histF", j) for j in range(NJF)],
                             2, C, [(of_p[:, c0:c0 + C], 0, 2)])
            for c0 in range(0, DFF, 1024):
                ts.append(lambda c0=c0: sp_(c0, min(1024, DFF - c0)))
        return [("ststore.q%d" % q, t) for t in ts]

    def ntiles(seg):
        return 4 if seg.kind == 'p' else 1

    def head_seg(q, seg):
        P.tag = "q%d.head.s%d" % (q, seg.idx)
        for t in range(ntiles(seg)):
            load_tile(q, seg, t)
        norm(seg, P_G1)

    def tail_seg(q, seg):
        P.tag = "q%d.tail.s%d" % (q, seg.idx)
        norm(seg, P_GF, to_hT=True)
        for t in range(ntiles(seg)):
            store_tile(q, seg, t)

    def head_tasks(q, seg):
        def tg(f):
            def run():
                P.tag = "q%d.head.s%d" % (q, seg.idx)
                f()
            return run
        ts = [tg(lambda t=t: load_tile(q, seg, t)) for t in range(ntiles(seg))]
        ts.append(tg(lambda: norm_a(seg)))
        ts.append(tg(lambda: norm_b(seg, P_G1)))
        return [("head.q%d.s%d" % (q, seg.idx), t) for t in ts]

    def tail_tasks(q, seg):
        def tg(f):
            def run():
                P.tag = "q%d.tail.s%d" % (q, seg.idx)
                f()
            return run
        ts = [tg(lambda: norm_a(seg)), tg(lambda: norm_b(seg, P_GF, to_hT=True))]
        ts += [tg(lambda t=t: store_tile(q, seg, t)) for t in range(ntiles(seg))]
        return [("tail.q%d.s%d" % (q, seg.idx), t) for t in ts]

    def start_pass_weights():
        for j in range(4):
            issue_win(j)
        for j in range(4):
            build_diag(j)

    SP0, SP1, SS = SEGS[0], SEGS[1], SEGS[2]

    def tg(kind, q, seg):
        P.tag = "q%d.%s.s%d" % (q, kind, seg.idx)

    def head_tiles(q, seg):
        tg("head", q, seg)
        for t in range(ntiles(seg)):
            head_tile(q, seg, t)

    def store_tiles(q, seg):
        tg("tail", q, seg)
        for t in range(ntiles(seg)):
            store_tile(q, seg, t)

    start_pass_weights()
    issue_xload(0, SP0)
    issue_xload(0, SP1)
    P.tag = "q0.states"
    for nm_, t_ in state_tasks(0):
        t_()
    head_tiles(0, SS)
    norm_a(SS)
    head_tiles(0, SP0)
    norm_a(SP0)
    tg("head", 0, SS)
    norm_b(SS, P_G1)
    head_tiles(0, SP1)
    norm_a(SP1)
    tg("head", 0, SP0)
    norm_b(SP0, P_G1)
    tg("head", 0, SP1)
    norm_b(SP1, P_G1)
    for q in range(NPASS):
        last = (q + 1 == NPASS)
        for seg in SEGS:
            for t in range(ntiles(seg)):
                BG.append(("ptile.q%d" % q, lambda seg=seg, t=t, q=q: load_p_tile(q, seg, t)))
        mixer_phase(q, [SS, SP0, SP1], bg_first=2,
                    hook=lambda seg, j: issue_wout() if (seg is SP1 and j == 1) else None)
        P.tag = "q%d.bgflush" % q
        bg_run(len(BG))
        load_gbc()
        for i in range(NWUP):
            issue_wup(i)
        pnd = None
        for seg in (SS, SP0, SP1):
            P.tag = "q%d.outproj.s%d" % (q, seg.idx)
            for m in range(KC):
                outproj(seg, m)
                if m == 3 and pnd is not None:
                    norm_b(pnd, P_G2)
                    pnd = None
            norm_a(seg)
            if pnd is not None:
                norm_b(pnd, P_G2)
            pnd = seg
        norm_b(pnd, P_G2)
        for i in range(NWDN):
            issue_wdn(i)
        for m in range(KC):
            issue_wpl(m)
        pend = None
        for h in range(2):
            for jj in range(NJH):
                j = h * NJH + jj
                P.tag = "q%d.ffnup.h%d" % (q, h)
                for seg in SEGS:
                    ctx = ffn_up(q, j, jj, seg)
                    if pend is not None:
                        ffn_up2(pend)
                    pend = ctx
                if j + NWUP < NJF:
                    issue_wup(j + NWUP)
            ffn_up2(pend)
            pend = None
            if h == 1 and not last:
                for j in range(4):
                    issue_win(j)
            P.tag = "q%d.ffndn.h%d" % (q, h)
            if h == 0:
                if not last:
                    BG.extend(state_tasks(q + 1)[0:4])
                for m in range(KC):
                    for seg in SEGS:
                        ffn_down(m, m, seg)
                    bg_run(1)
                    issue_wdn(m + NWDN)
            else:
                if not last:
                    BG.extend(state_tasks(q + 1)[4:7])
                BG.extend(state_store_tasks(q))
                for m in range(4):
                    for seg in SEGS:
                        ffn_down(8 + m, m, seg)
                        bg_run(1)
                    issue_wdn(8 + m + NWDN)
                pnd = None
                for seg in (SS, SP0, SP1):
                    for m in range(4, 8):
                        ffn_down(8 + m, m, seg)
                        bg_run(1)
                        if m == 6 and pnd is not None:
                            norm_b(pnd, P_G3)
                            pnd = None
                    norm_a(seg)
                    if pnd is not None:
                        norm_b(pnd, P_G3)
                    pnd = seg
                late3 = pnd
        if not last:
            issue_xload(q + 1, SP0)
            issue_xload(q + 1, SP1)

        def ple_seg(seg, ms):
            P.tag = "q%d.ple" % q
            for m in ms:
                ple(m, seg)

        ST = []

        def st_run(k):
            P.tag = "q%d.states" % (q + 1)
            for _ in range(k):
                if ST:
                    ST.pop(0)()

        P.tag = "q%d.ple" % q
        for m in range(4):
            ple(m, SS)
            ple(m, SP0)
            if m == 1:
                norm_b(late3, P_G3)
                P.tag = "q%d.ple" % q
        for m in range(4, 8):
            ple(m, SS)
        tg("tail", q, SS)
        norm_a(SS)
        ple_seg(SP0, range(4, 6))
        tail_stats(q, SS)
        ple_seg(SP0, range(6, 8))
        tg("tail", q, SP0)
        norm_a(SP0)
        store_tile_norm(q, SS, 0)
        ple_seg(SP1, range(0, 2))
        tail_stats(q, SP0)
        if not last:
            head_tiles(q + 1, SS)
            norm_a(SS)
        ple_seg(SP1, range(2, 4))
        store_tile_norm(q, SP0, 0)
        st_run(2)
        ple_seg(SP1, range(4, 6))
        store_tile_norm(q, SP0, 1)
        ple_seg(SP1, range(6, 8))
        tg("tail", q, SP1)
        norm_a(SP1)
        store_tile_norm(q, SP0, 2)
        st_run(len(ST))
        store_tile_norm(q, SP0, 3)
        if not last:
            for j in range(4):
                build_diag(j)
            tg("head", q + 1, SS)
            norm_b(SS, P_G1)
        tail_stats(q, SP1)
        if not last:
            head_tiles(q + 1, SP0)
            norm_a(SP0)
        for t in range(4):
            store_tile_norm(q, SP1, t)
            if t == 1 and not last:
                tg("head", q + 1, SP0)
                norm_b(SP0, P_G1)
        if not last:
            head_tiles(q + 1, SP1)
            norm_a(SP1)
            norm_b(SP1, P_G1)
        P.tag = "q%d.ststate" % q
        bg_run(len(BG))

    P.fence_all_dma()

    with nc.Block() as block:
        @block.tensor
        def _(e):
            P.replay('pe', e)

        @block.vector
        def _(e):
            P.replay('dve', e)

        @block.scalar
        def _(e):
            P.replay('act', e)

        @block.gpsimd
        def _(e):
            P.replay('pool', e)

        @block.sync
        def _(e):
            P.replay('sp', e)
    LAST_PE_LABELS[:] = P.pe_labels
    return nc


def _prep_shared(inp):
    f = lambda a: np.ascontiguousarray(a, dtype=np.float32)
    w_in = inp["w_in"][0].reshape(8, 128, 5, 4, 128).transpose(3, 1, 0, 2, 4).reshape(4, 128, KC * 640)
    w_out = inp["w_out"][0].reshape(8, 128, D).transpose(1, 0, 2).reshape(128, KC * D)
    wu = inp["w_up"][0]
    gate = wu[:, :DFF].reshape(8, 128, NJF, 128)
    val = wu[:, DFF:].reshape(8, 128, NJF, 128)
    w_up = np.stack([gate, val], axis=3).transpose(2, 1, 0, 3, 4).reshape(NJF, 128, KC * 256)
    w_dn = inp["w_down"][0].reshape(2, NJH, 128, 8, 128).transpose(0, 3, 2, 1, 4).reshape(2, 8, 128, NJH * 128)
    wg = inp["w_ple_gate"][0].reshape(8, 128, 8, 128).transpose(2, 1, 0, 3)
    wp = inp["w_ple"][0].reshape(2, 128, 8, 128).transpose(2, 1, 0, 3)
    w_pl = np.concatenate([wg, wp], axis=2).reshape(8, 128, 10 * 128)
    cols = [
        inp["g_mix"][0].reshape(8, 128).T, inp["g_ffn"][0].reshape(8, 128).T,
        inp["g_ple"][0].reshape(8, 128).T, inp["g_final"].reshape(8, 128).T,
        inp["conv_a_w"][0].reshape(3, 4, 128).transpose(2, 1, 0).reshape(128, 12),
        inp["conv_b_w"][0].reshape(31, 4, 128).transpose(2, 1, 0).reshape(128, 124),
        inp["conv_b_b"][0].reshape(4, 128).T, inp["ln_b_g"][0].reshape(4, 128).T,
        inp["ln_b_b"][0].reshape(4, 128).T,
        inp["conv_f_w"][0].reshape(3, NJF, 128).transpose(2, 1, 0).reshape(128, 66),
        inp["conv_f_b"][0].reshape(NJF, 128).T,
    ]
    prm = np.concatenate(cols, axis=1)
    assert prm.shape == (128, NPRM)
    return {"w_in": f(w_in), "w_out": f(w_out), "w_up": f(w_up), "w_dn": f(w_dn), "w_pl": f(w_pl), "prm": f(prm)}


def _core_inputs(inp, shared, c):
    f = lambda a: np.ascontiguousarray(a, dtype=np.float32)
    d = dict(shared)
    d["xp"] = f(inp["x_prompt"][c])
    d["xs"] = f(inp["x_sample"][16 * c:16 * c + 16].reshape(128, D))
    d["pp"] = f(inp["p_prompt"][0, c])
    d["ps"] = f(inp["p_sample"][0, 16 * c:16 * c + 16].reshape(128, 256))
    d["stA"] = f(inp["state_conv_a"][0, 16 * c:16 * c + 16].reshape(32, 512))
    d["stB"] = f(inp["state_conv_b"][0, 16 * c:16 * c + 16])
    d["stF"] = f(inp["state_ffn_conv"][0, 16 * c:16 * c + 16].reshape(32, DFF))
    d["gfin"] = f(inp["g_final"].reshape(1, D))
    return d


def _assemble(results):
    y_p = np.stack([r["y_p"] for r in results], axis=0)
    y_s = np.concatenate([r["y_s"].reshape(16, 8, D) for r in results], axis=0)
    oa_p = np.stack([r["oa_p"] for r in results], axis=0)[None]
    ob_p = np.stack([r["ob_p"] for r in results], axis=0)[None]
    of_p = np.stack([r["of_p"] for r in results], axis=0)[None]
    oa_s = np.concatenate([r["oa_s"].reshape(16, 2, 512) for r in results], axis=0)[None]
    ob_s = np.concatenate([r["ob_s"] for r in results], axis=0)[None]
    of_s = np.concatenate([r["of_s"].reshape(16, 2, DFF) for r in results], axis=0)[None]
    outs = (y_p, y_s, oa_p, ob_p, of_p, oa_s, ob_s, of_s)
    return tuple(np.ascontiguousarray(o, dtype=np.float32) for o in outs)


def kernel(**inputs):
    inp = {k: np.asarray(v) for k, v in inputs.items()}
    shared = _prep_shared(inp)
    in_maps = [_core_inputs(inp, shared, c) for c in range(NCORES)]
    nc = build_nc()
    res = run_bass_kernel_spmd(nc, in_maps, core_ids=list(range(NCORES)))
    return _assemble(res.results)
```
